# Optimizing a Trainium2 kernel written in Bass

```python
import math
import jax, jax.numpy as jnp
from jax import lax
import numpy as np

D_MODEL = 2048
BATCH = 2
SEQ = 8192
DEPTH = 2

N_EVEN = (DEPTH + 1) // 2
N_ODD = DEPTH // 2
NORM_EPS = 1e-6

ATTN_GROUPS = ((128, 1), (512, 4), (2048, 16))
ATTN_HEADS_PER_GROUP = 8
ATTN_HEAD_DIM = 64
ATTN_BLOCK = 128
ATTN_GROUP_WIDTH = ATTN_HEADS_PER_GROUP * ATTN_HEAD_DIM
ATTN_QKV_WIDTH = 3 * len(ATTN_GROUPS) * ATTN_GROUP_WIDTH
ATTN_OUT_WIDTH = ATTN_GROUP_WIDTH

POOL_WINDOWS = (2, 4, 8, 16)
POOL_GROUP_WIDTH = D_MODEL // 16
POOL_WIDTH = len(POOL_WINDOWS) * POOL_GROUP_WIDTH

EVEN_IN_WIDTH = ATTN_QKV_WIDTH + POOL_WIDTH
EVEN_OUT_WIDTH = ATTN_OUT_WIDTH + POOL_WIDTH

SSM_EXPAND = 2
SSM_D_INNER = SSM_EXPAND * D_MODEL
SSM_HEAD_DIM = 64
SSM_HEADS = SSM_D_INNER // SSM_HEAD_DIM
SSM_GROUPS = 8
SSM_STATE = 128
SSM_CONV = 4
SSM_CHUNK = 128
SSM_CONV_DIM = SSM_D_INNER + 2 * SSM_GROUPS * SSM_STATE
SSM_IN_WIDTH = SSM_D_INNER + SSM_CONV_DIM + SSM_HEADS

FFN_HIDDEN = 4 * D_MODEL

kernel_name = "hybrid_dilated_attn_pool_ssd_adaln"


def rmsnorm(x, g):
    xf = x.astype(jnp.float32)
    y = xf * lax.rsqrt(jnp.mean(xf * xf, axis=-1, keepdims=True) + NORM_EPS)
    return (y * g.astype(jnp.float32)).astype(x.dtype)


def modulate(h, shift, scale):
    return h * (1 + scale[:, None, :]) + shift[:, None, :]


def dilated_window_attention(q, k, v, dilation, n_back):
    b, s, h, e = q.shape
    L = s // dilation
    nb = -(-L // ATTN_BLOCK)
    Lp = nb * ATTN_BLOCK

    def to_sub(t):
        t = t.reshape(b, L, dilation, h, e).transpose(0, 2, 3, 1, 4)
        return jnp.pad(t, ((0, 0), (0, 0), (0, 0), (0, Lp - L), (0, 0)))

    def windows(t):
        t = jnp.pad(to_sub(t), ((0, 0), (0, 0), (0, 0), (ATTN_BLOCK, 0), (0, 0)))
        t = t.reshape(b, dilation, h, nb + 1, ATTN_BLOCK, e)
        return jnp.concatenate([t[:, :, :, :-1], t[:, :, :, 1:]], axis=4)

    qs = to_sub(q).reshape(b, dilation, h, nb, ATTN_BLOCK, e)
    ks, vs = windows(k), windows(v)
    scores = jnp.einsum('bdhnqe,bdhnke->bdhnqk', qs, ks).astype(jnp.float32) * (e ** -0.5)
    qi = jnp.arange(ATTN_BLOCK)[:, None]
    kj = jnp.arange(2 * ATTN_BLOCK)[None, :]
    dist = qi + ATTN_BLOCK - kj
    kpos = jnp.arange(nb)[:, None, None] * ATTN_BLOCK - ATTN_BLOCK + kj[None]
    mask = (dist >= 0) & (dist <= n_back) & (kpos >= 0)
    scores = jnp.where(mask, scores, -jnp.inf)
    m = jnp.max(scores, axis=-1, keepdims=True)
    p = jnp.exp(scores - m)
    den = jnp.sum(p, axis=-1, keepdims=True)
    o = jnp.einsum('bdhnqk,bdhnke->bdhnqe', (p / den).astype(v.dtype), vs)
    lse = (m + jnp.log(den))[..., 0]

    def from_sub(t):
        t = t.reshape(b, dilation, h, Lp, *t.shape[5:])[:, :, :, :L]
        t = jnp.moveaxis(t, 3, 1)
        return t.reshape(b, s, h, *t.shape[4:])

    return from_sub(o), from_sub(lse)


def multiscale_pool(u, pool_w, pool_scale):
    b, s, _ = u.shape
    ug = u.reshape(b, s, len(POOL_WINDOWS), POOL_GROUP_WIDTH).astype(jnp.float32)
    cs = jnp.pad(jnp.cumsum(ug, axis=1), ((0, 0), (1, 0), (0, 0), (0, 0)))
    t = jnp.arange(s)
    diffs = []
    for gi, w in enumerate(POOL_WINDOWS):
        csg = cs[:, :, gi]
        lo = jnp.maximum(t + 1 - w, 0)
        cnt = jnp.minimum(t + 1, w).astype(jnp.float32)
        mean = (csg[:, 1:] - csg[:, lo]) / cnt[None, :, None]
        diffs.append(mean - ug[:, :, gi])
    d = jnp.stack(diffs, axis=2).astype(u.dtype)
    y = jnp.einsum('bsgc,gce->bsge', d, pool_w)
    return y.reshape(b, s, POOL_WIDTH) * pool_scale


def attn_pool_mixer(h, w_in, pool_w, pool_scale, w_out):
    b, s, _ = h.shape
    proj = h @ w_in
    qkv = proj[..., :ATTN_QKV_WIDTH].reshape(b, s, 3, len(ATTN_GROUPS), ATTN_HEADS_PER_GROUP, ATTN_HEAD_DIM)
    u = proj[..., ATTN_QKV_WIDTH:]
    outs, lses = [], []
    for gi, (window, dil) in enumerate(ATTN_GROUPS):
        o, lse = dilated_window_attention(qkv[:, :, 0, gi], qkv[:, :, 1, gi], qkv[:, :, 2, gi], dil, window // dil)
        outs.append(o)
        lses.append(lse)
    wts = jax.nn.softmax(jnp.stack(lses, axis=0), axis=0)
    attn = jnp.sum(wts[..., None] * jnp.stack(outs, axis=0).astype(jnp.float32), axis=0)
    attn = attn.astype(h.dtype).reshape(b, s, ATTN_OUT_WIDTH)
    pool = multiscale_pool(u, pool_w, pool_scale)
    return jnp.concatenate([attn, pool], axis=-1) @ w_out


def causal_depthwise_conv(x, w, bias):
    y = lax.conv_general_dilated(x, w[:, None, :], window_strides=(1,), padding=[(SSM_CONV - 1, 0)],
                                 dimension_numbers=('NWC', 'WIO', 'NWC'), feature_group_count=x.shape[-1])
    return y + bias


def ssd_scan(x, dt, A, bm, cm):
    b, s, h, p = x.shape
    g, n = bm.shape[2], bm.shape[3]
    r = h // g
    q = SSM_CHUNK
    nc = s // q
    xc = x.astype(jnp.float32).reshape(b, nc, q, g, r, p)
    bc = bm.astype(jnp.float32).reshape(b, nc, q, g, n)
    cc = cm.astype(jnp.float32).reshape(b, nc, q, g, n)
    dtc = dt.reshape(b, nc, q, g, r)
    a = jnp.cumsum(dtc * A.reshape(g, r), axis=2)
    xdt = xc * dtc[..., None]
    causal = jnp.tril(jnp.ones((q, q), dtype=bool))[:, :, None, None]
    decay = jnp.exp(jnp.where(causal, a[:, :, :, None] - a[:, :, None], -jnp.inf))
    cb = jnp.einsum('bclgn,bcsgn->bclsg', cc, bc)
    y_diag = jnp.einsum('bclsgr,bcsgrp->bclgrp', cb[..., None] * decay, xdt)
    decay_to_end = jnp.exp(a[:, :, -1:] - a)
    states = jnp.einsum('bcsgn,bcsgrp->bcgrpn', bc, xdt * decay_to_end[..., None])
    chunk_decay = jnp.exp(a[:, :, -1])

    def step(carry, inp):
        st, dec = inp
        return carry * dec[..., None, None] + st, carry

    init = jnp.zeros((b, g, r, p, n), jnp.float32)
    _, h_in = lax.scan(step, init, (jnp.moveaxis(states, 1, 0), jnp.moveaxis(chunk_decay, 1, 0)))
    h_in = jnp.moveaxis(h_in, 0, 1)
    y_off = jnp.einsum('bclgn,bcgrpn->bclgrp', cc, h_in) * jnp.exp(a)[..., None]
    return (y_diag + y_off).reshape(b, s, h, p)


def gated_group_rmsnorm(y, z, g):
    yf = y.astype(jnp.float32) * jax.nn.silu(z.astype(jnp.float32))
    b, s, dim = yf.shape
    yg = yf.reshape(b, s, SSM_GROUPS, dim // SSM_GROUPS)
    yg = yg * lax.rsqrt(jnp.mean(yg * yg, axis=-1, keepdims=True) + NORM_EPS)
    return (yg.reshape(b, s, dim) * g.astype(jnp.float32)).astype(z.dtype)


def ssd_mixer(h, w_in, conv_w, conv_b, dt_bias, a_log, d_skip, norm_g, w_out):
    b, s, _ = h.shape
    proj = h @ w_in
    z = proj[..., :SSM_D_INNER]
    xbc = proj[..., SSM_D_INNER:SSM_D_INNER + SSM_CONV_DIM]
    dt = proj[..., SSM_D_INNER + SSM_CONV_DIM:]
    xbc = jax.nn.silu(causal_depthwise_conv(xbc, conv_w, conv_b))
    xs = xbc[..., :SSM_D_INNER].reshape(b, s, SSM_HEADS, SSM_HEAD_DIM)
    bm = xbc[..., SSM_D_INNER:SSM_D_INNER + SSM_GROUPS * SSM_STATE].reshape(b, s, SSM_GROUPS, SSM_STATE)
    cm = xbc[..., SSM_D_INNER + SSM_GROUPS * SSM_STATE:].reshape(b, s, SSM_GROUPS, SSM_STATE)
    dt = jax.nn.softplus((dt + dt_bias).astype(jnp.float32))
    A = -jnp.exp(a_log.astype(jnp.float32))
    y = ssd_scan(xs, dt, A, bm, cm)
    y = y + d_skip.astype(jnp.float32)[:, None] * xs.astype(jnp.float32)
    y = gated_group_rmsnorm(y.reshape(b, s, SSM_D_INNER), z, norm_g)
    return y @ w_out


def squared_relu_mlp(h, w1, w2):
    return jnp.square(jax.nn.relu(h @ w1)) @ w2


def setup_inputs(seed: int = 0) -> dict:
    key = jax.random.key(seed)
    ks = jax.random.split(key, 24)
    f32 = jnp.float32
    nrm = lambda k, shape, scale: jax.random.normal(k, shape, f32) * scale
    D = D_MODEL
    dt0 = jnp.exp(jax.random.uniform(ks[16], (N_ODD, SSM_HEADS), f32, minval=math.log(1e-3), maxval=math.log(1e-1)))
    return {
        "x": nrm(ks[0], (BATCH, SEQ, D), 1.0),
        "c": nrm(ks[1], (BATCH, D), 1.0),
        "ada_w": nrm(ks[2], (DEPTH, D, 6 * D), 0.5 * D ** -0.5),
        "ada_b": nrm(ks[3], (DEPTH, 6 * D), 0.02),
        "norm_mix": 1.0 + nrm(ks[4], (DEPTH, D), 0.02),
        "norm_ffn": 1.0 + nrm(ks[5], (DEPTH, D), 0.02),
        "ffn_w1": nrm(ks[6], (DEPTH, D, FFN_HIDDEN), D ** -0.5),
        "ffn_w2": nrm(ks[7], (DEPTH, FFN_HIDDEN, D), FFN_HIDDEN ** -0.5),
        "even_w_in": nrm(ks[8], (N_EVEN, D, EVEN_IN_WIDTH), D ** -0.5),
        "pool_w": nrm(ks[9], (N_EVEN, len(POOL_WINDOWS), POOL_GROUP_WIDTH, POOL_GROUP_WIDTH), POOL_GROUP_WIDTH ** -0.5),
        "pool_scale": 1.0 + nrm(ks[10], (N_EVEN, POOL_WIDTH), 0.02),
        "even_w_out": nrm(ks[11], (N_EVEN, EVEN_OUT_WIDTH, D), EVEN_OUT_WIDTH ** -0.5),
        "ssm_w_in": nrm(ks[12], (N_ODD, D, SSM_IN_WIDTH), D ** -0.5),
        "ssm_conv_w": nrm(ks[13], (N_ODD, SSM_CONV, SSM_CONV_DIM), SSM_CONV ** -0.5),
        "ssm_conv_b": nrm(ks[14], (N_ODD, SSM_CONV_DIM), 0.02),
        "ssm_dt_bias": dt0 + jnp.log(-jnp.expm1(-dt0)),
        "ssm_a_log": jnp.log(jax.random.uniform(ks[15], (N_ODD, SSM_HEADS), f32, minval=1.0, maxval=16.0)),
        "ssm_d": 1.0 + nrm(ks[17], (N_ODD, SSM_HEADS), 0.02),
        "ssm_norm": 1.0 + nrm(ks[18], (N_ODD, SSM_D_INNER), 0.02),
        "ssm_w_out": nrm(ks[19], (N_ODD, SSM_D_INNER, D), SSM_D_INNER ** -0.5),
        "final_norm": 1.0 + nrm(ks[20], (D,), 0.02),
    }


def reference(x, c, ada_w, ada_b, norm_mix, norm_ffn, ffn_w1, ffn_w2, even_w_in, pool_w, pool_scale,
              even_w_out, ssm_w_in, ssm_conv_w, ssm_conv_b, ssm_dt_bias, ssm_a_log, ssm_d, ssm_norm,
              ssm_w_out, final_norm):
    cond = jax.nn.silu(c)
    for i in range(DEPTH):
        mod = cond @ ada_w[i] + ada_b[i]
        sh1, sc1, g1, sh2, sc2, g2 = jnp.split(mod, 6, axis=-1)
        h = modulate(rmsnorm(x, norm_mix[i]), sh1, sc1)
        j = i // 2
        if i % 2 == 0:
            y = attn_pool_mixer(h, even_w_in[j], pool_w[j], pool_scale[j], even_w_out[j])
        else:
            y = ssd_mixer(h, ssm_w_in[j], ssm_conv_w[j], ssm_conv_b[j], ssm_dt_bias[j], ssm_a_log[j],
                          ssm_d[j], ssm_norm[j], ssm_w_out[j])
        x = x + g1[:, None, :] * y
        h = modulate(rmsnorm(x, norm_ffn[i]), sh2, sc2)
        x = x + g2[:, None, :] * squared_relu_mlp(h, ffn_w1[i], ffn_w2[i])
    return rmsnorm(x, final_norm)
```

```python
import contextlib
import os
import numpy as np
import concourse.bass as bass
import concourse.mybir as mybir
from concourse.bass_utils import run_bass_kernel_spmd

F32 = mybir.dt.float32
BF16 = mybir.dt.bfloat16
AF = mybir.ActivationFunctionType
ALU = mybir.AluOpType
AX = mybir.AxisListType

ENGS = ["pe", "act", "dve", "pool", "sp"]
DMA_RING = 12
SAME_ENGINE_SYNC = True


class Op:
    __slots__ = ("eng", "fn", "deps", "is_dma", "sem", "val", "signal", "idx", "tag")

    def __init__(self, eng, fn, is_dma=False, tag=""):
        self.eng = eng
        self.fn = fn
        self.deps = []
        self.is_dma = is_dma
        self.sem = None
        self.val = None
        self.signal = False
        self.idx = -1
        self.tag = tag


class Res:
    __slots__ = ("writers", "readers", "war")

    def __init__(self):
        self.writers = []
        self.readers = []
        self.war = []


def _key(x):
    if isinstance(x, str):
        return x
    if isinstance(x, tuple):
        return _key(x[0]) + "#" + str(x[1])
    return x.name


class Prog:
    def __init__(self, nc):
        self.nc = nc
        self.ops = {e: [] for e in ENGS}
        self.res = {}
        self.dma_count = {e: 0 for e in ENGS}
        self.dma_ops = {e: [] for e in ENGS}
        self.pending = {e: None for e in ENGS}
        self.psum_names = set()

    def _res(self, k):
        r = self.res.get(k)
        if r is None:
            r = self.res[k] = Res()
        return r

    @staticmethod
    def _add(lst, op):
        if not op.is_dma:
            lst[:] = [o for o in lst if o.is_dma or o.eng != op.eng]
        lst.append(op)

    def barrier(self):
        deps = []
        for e in ENGS:
            last = None
            for o in reversed(self.ops[e]):
                if not o.is_dma:
                    last = o
                    break
            if last is not None:
                deps.append(last)
            deps += self.dma_ops[e][-DMA_RING:]
        for e in ENGS:
            self.pending[e] = list(deps)
        self.res = {}

    def _track(self, op, reads, writes, wparts):
        deps = []
        if self.pending[op.eng] is not None:
            deps += self.pending[op.eng]
            self.pending[op.eng] = None
        for r in reads:
            kr = _key(r)
            R = self._res(kr)
            deps += R.writers
            if kr.split("#")[0] in self.psum_names:
                deps += R.readers
        for w in writes:
            R = self._res(_key(w))
            deps += R.writers + R.readers + R.war
        for w in wparts:
            R = self._res(_key(w))
            deps += R.readers + R.war
        for r in reads:
            self._add(self._res(_key(r)).readers, op)
        for w in writes:
            R = self._res(_key(w))
            R.writers = [op]
            R.readers = []
            R.war = []
        for w in wparts:
            R = self._res(_key(w))
            if R.readers:
                R.war = R.readers
                R.readers = []
                R.writers = [op]
            else:
                self._add(R.writers, op)
        seen = set()
        for d in deps:
            if d is op or id(d) in seen:
                continue
            seen.add(id(d))
            if d.eng == op.eng and not d.is_dma and not op.is_dma:
                if op.eng == "pe" or not SAME_ENGINE_SYNC:
                    continue
            op.deps.append(d)

    def op(self, eng, fn, reads=(), writes=(), wparts=(), tag=""):
        o = Op(eng, fn, tag=tag)
        self._track(o, reads, writes, wparts)
        o.idx = len(self.ops[eng])
        self.ops[eng].append(o)
        return o

    def dma(self, eng, out, in_, reads=None, writes=None, wparts=(), **kw):
        if reads is None:
            reads = [in_]
        if writes is None and not wparts:
            writes = [out]
        writes = writes or []

        def fn(e, out=out, in_=in_, kw=kw):
            return e.dma_start(out=out, in_=in_, **kw)

        o = Op(eng, fn, is_dma=True)
        n = self.dma_count[eng]
        self.dma_count[eng] = n + 1
        o.sem = (eng, n % DMA_RING)
        o.val = 16 * (n // DMA_RING + 1)
        if n >= DMA_RING:
            o.deps.append(self.dma_ops[eng][n - DMA_RING])
        self.dma_ops[eng].append(o)
        self._track(o, reads, writes, wparts)
        o.idx = len(self.ops[eng])
        self.ops[eng].append(o)
        return o

    def emit(self):
        nc = self.nc
        for e in ENGS:
            for o in self.ops[e]:
                for d in o.deps:
                    if not d.is_dma:
                        d.signal = True
        lastsig = {}
        for e in ENGS:
            c = 0
            last = None
            for o in self.ops[e]:
                if not o.is_dma:
                    last = o
            if last is not None:
                last.signal = True
            for o in self.ops[e]:
                if not o.is_dma and o.signal:
                    c += 1
                    o.sem = ("eng", e)
                    o.val = c
            lastsig[e] = last
        with contextlib.ExitStack() as st:
            sems = {}
            for e in ENGS:
                sems[("eng", e)] = st.enter_context(nc.semaphore("s_" + e))
            for e in ENGS:
                for i in range(min(DMA_RING, self.dma_count[e])):
                    sems[(e, i)] = st.enter_context(nc.semaphore(f"d_{e}{i}"))
            block = st.enter_context(nc.Block())
            tail = []
            for e in ENGS:
                tail += self.dma_ops[e][-DMA_RING:]

            def run(engname, eng):
                known = {}
                for o in self.ops[engname]:
                    need = {}
                    for d in o.deps:
                        if need.get(d.sem, 0) < d.val:
                            need[d.sem] = d.val
                    for s, v in need.items():
                        if known.get(s, 0) < v:
                            eng.wait_ge(sems[s], v)
                            known[s] = v
                    ins = o.fn(eng)
                    if o.is_dma:
                        ins.then_inc(sems[o.sem], 16)
                    elif o.signal:
                        ins.then_inc(sems[o.sem], 1)
                if engname == "sp":
                    for d in tail:
                        if known.get(d.sem, 0) < d.val:
                            eng.wait_ge(sems[d.sem], d.val)
                            known[d.sem] = d.val
                    for e in ENGS:
                        last = lastsig[e]
                        if last is not None and e != "sp":
                            eng.wait_ge(sems[last.sem], last.val)

            @block.tensor
            def _(eng):
                run("pe", eng)

            @block.scalar
            def _(eng):
                run("act", eng)

            @block.vector
            def _(eng):
                run("dve", eng)

            @block.gpsimd
            def _(eng):
                run("pool", eng)

            @block.sync
            def _(eng):
                run("sp", eng)

    def stats(self):
        return {e: len(self.ops[e]) for e in ENGS}


D = 2048
KC = 16
T = 2048
HALO = 2048
SEQ = 8192
NCORE = 8
EPS = 1e-6
GROUPS = ((128, 1), (512, 4), (2048, 16))
QKV_W = 4608
IN_W0 = 5120
FFN_H = 8192
NEG = -30000.0
DI = 4096
SSM_IN = 10304


class Rot:
    def __init__(self, items):
        self.items = items
        self.i = 0

    def next(self):
        x = self.items[self.i % len(self.items)]
        self.i += 1
        return x


class Builder:
    def __init__(self, stage):
        self.stage = stage
        self.nc = bass.Bass("TRN2", target_bir_lowering=False)
        self.P = Prog(self.nc)
        self.scopes = []
        self.uid = 0
        self.ins = {}

    def din(self, name, shape, dt=F32):
        t = self.nc.dram_tensor(name, list(shape), dt, kind="ExternalInput").ap()
        self.ins[name] = t
        return t

    def dout(self, name, shape, dt=F32):
        return self.nc.dram_tensor(name, list(shape), dt, kind="ExternalOutput").ap()

    def dscr(self, name, shape, dt):
        return self.nc.dram_tensor(name, list(shape), dt, kind="Internal").ap()

    def push(self):
        st = contextlib.ExitStack()
        self.scopes.append(st)
        return st

    def pop(self):
        self.P.barrier()
        self.scopes.pop().close()

    def sb(self, name, shape, dt):
        self.uid += 1
        return self.scopes[-1].enter_context(self.nc.sbuf_tensor(f"{name}_{self.uid}", list(shape), dt))

    def ps(self, name, shape, dt=F32):
        self.uid += 1
        nm = f"{name}_{self.uid}"
        self.P.psum_names.add(nm)
        return self.scopes[-1].enter_context(self.nc.psum_tensor(nm, list(shape), dt))

    def const(self, arr, name):
        return self.nc.inline_tensor(np.ascontiguousarray(arr), name).ap()

    def build(self):
        P = self.P
        stage = self.stage
        c_in = self.din("c_in", [D])
        hflag_in = self.din("hflag", [128, 1])
        ada_w = self.din("ada_w", [2, D, 6 * D])
        ada_b = self.din("ada_b", [2, 6 * D])
        norm_mix = self.din("norm_mix", [2, D])
        norm_ffn = self.din("norm_ffn", [2, D])
        ffn_w1 = self.din("ffn_w1", [2, D, FFN_H])
        ffn_w2 = self.din("ffn_w2", [2, FFN_H, D])
        self.norm_mix, self.norm_ffn = norm_mix, norm_ffn
        mod_d = self.dscr("mod_d", [2, 6 * D], F32)
        self.mod_d = mod_d
        hid_d = self.dscr("hid_d", [FFN_H, T], BF16)

        self.push()
        ident_b = self.sb("identb", [128, 128], BF16)
        ident_f = self.sb("identf", [128, 128], F32)
        hflag = self.sb("hflag", [128, 1], F32)
        self.ident_b, self.ident_f = ident_b, ident_f
        idn = self.const(np.eye(128, dtype=np.float32), "idn")
        P.dma("sp", ident_f[:], idn[:, :])
        P.op("dve", lambda e: e.tensor_copy(ident_b[:], ident_f[:]), reads=[ident_f], writes=[ident_b])
        P.dma("sp", hflag[:], hflag_in[:, :])
        self.wt = Rot([self.sb(f"wt{i}", [128, 8192], BF16) for i in range(2)])

        if stage in ("L0", "x1"):
            x_ext = self.din("x_ext", [HALO + T, D])
            hmask_in = self.din("hmask", [128, 256])
            invc_in = self.din("invc", [128, 4, 16])
            even_w_in = self.din("even_w_in", [1, D, IN_W0])
            pool_w = self.din("pool_w", [1, 4, 128, 128])
            pool_scale = self.din("pool_scale", [1, 512])
            even_w_out = self.din("even_w_out", [1, 1024, D])
            self.x_ext = x_ext
            qkv_d = self.dscr("qkv_d", [HALO + T, QKV_W], BF16)
            att_d = self.dscr("att_d", [3, T, 520], F32)
            x1_d = self.dscr("x1_d", [T, D], F32)
            out_x = self.dout("out_x", [T, D])
            band = self.sb("band", [128, 256], F32)
            hmask = self.sb("hmask", [128, 256], F32)
            qi = np.arange(128)[:, None]
            kj = np.arange(256)[None, :]
            dist = qi + 128 - kj
            bandm = np.where((dist >= 0) & (dist <= 128), 0.0, NEG).astype(np.float32)
            P.dma("sp", band[:], self.const(bandm, "bandc")[:, :])
            P.dma("sp", hmask[:], hmask_in[:, :])
            self.phase_ada(c_in, ada_w, ada_b, mod_d, layers=(0,))
            self.l0_mixer(x_ext, even_w_in, pool_w, pool_scale, even_w_out, qkv_d, att_d, x1_d, hmask, band, hflag, invc_in)
            if stage == "x1":
                self.copy_out(x1_d, out_x)
            else:
                self.ffn(0, x1_d, out_x, ffn_w1, ffn_w2, hid_d)
        else:
            mode = "full" if stage in ("L1f", "x3") else "states"
            x_own = self.din("x_own", [T, D])
            x_halo = self.din("x_halo", [128, D])
            ssm_w_in = self.din("ssm_w_in", [1, D, SSM_IN])
            conv_w = self.din("ssm_conv_w", [1, 4, 6144])
            conv_b = self.din("ssm_conv_b", [1, 6144])
            dt_bias = self.din("ssm_dt_bias", [1, 64])
            a_log = self.din("ssm_a_log", [1, 64])
            d_skip = self.din("ssm_d", [1, 64])
            xs_d = self.dscr("xs_d", [T, DI], BF16)
            b_d = self.dscr("b_d", [T, 1024], BF16)
            bT_d = self.dscr("bT_d", [1024, T], BF16)
            cT_d = self.dscr("cT_d", [1024, T], BF16)
            z_d = self.dscr("z_d", [T, DI], BF16)
            yn_d = self.dscr("yn_d", [T, DI], BF16)
            if "noada" not in os.environ.get("L1DBG", ""):
                self.phase_ada(c_in, ada_w, ada_b, mod_d, layers=(1,))
            self.push()
            C = self.l1_consts(dt_bias, a_log, d_skip)
            dtall = self.sb("dtall", [128, 16, 64], F32)
            if "noinproj" not in os.environ.get("L1DBG", ""):
                self.l1_inproj(mode, x_own, x_halo, hflag, ssm_w_in, conv_w, conv_b, C, dtall, xs_d, b_d, bT_d, cT_d, z_d)
            state = self.sb("state", [128, DI], F32)
            state_bf = self.sb("statebf", [128, DI], BF16)
            if mode == "states":
                S_out = self.dout("S_loc", [128, DI])
                L_out = self.dout("Ltot", [1, 64])
                Lacc = self.sb("Lacc", [128, 64], F32)
                P.op("pool", lambda e: e.memset(state[:], 0.0), writes=[state])
                P.op("pool", lambda e: e.memset(Lacc[:], 0.0), writes=[Lacc])
                P.barrier()
                if "noscan" not in os.environ.get("L1DBG", ""):
                    self.l1_scan(mode, C, dtall, state, state_bf, Lacc, xs_d, b_d, bT_d, cT_d, z_d, yn_d, None)
                P.dma("sp", S_out[:, :], state[:])
                P.dma("sp", L_out[:, :], Lacc[0:1, :])
                self.pop()
            else:
                ssm_norm = self.din("ssm_norm", [1, DI])
                ssm_w_out = self.din("ssm_w_out", [1, DI, D])
                final_norm_w = self.din("final_norm", [D])
                Spred = self.din("Spred", [3, 128, DI])
                Lpred = self.din("Lpred", [3, 64])
                x3_d = self.dscr("x3_d", [T, D], F32)
                x4_d = self.dscr("x4_d", [T, D], F32)
                out_x = self.dout("out_x", [T, D])
                gn_bc = self.sb("gnbc", [128, DI], F32)
                P.dma("sp", gn_bc[:], ssm_norm[0, :].partition_broadcast(128))
                self.l1_combine(Spred, Lpred, state, state_bf)
                self.l1_scan(mode, C, dtall, state, state_bf, None, xs_d, b_d, bT_d, cT_d, z_d, yn_d, gn_bc)
                self.pop()
                if stage == "x3":
                    self.l1_outproj(yn_d, ssm_w_out, x_own, out_x)
                else:
                    self.l1_outproj(yn_d, ssm_w_out, x_own, x3_d)
                    self.ffn(1, x3_d, x4_d, ffn_w1, ffn_w2, hid_d)
                    self.final_norm(x4_d, final_norm_w, out_x)
        self.pop()
        P.emit()
        return self.nc

    def copy_out(self, src, dst):
        P = self.P
        self.push()
        bufs = Rot([self.sb(f"cp{i}", [128, D], F32) for i in range(2)])
        for t in range(T // 128):
            b = bufs.next()
            P.dma("sp", b[:], src[t * 128:(t + 1) * 128, :])
            P.dma("sp", dst[t * 128:(t + 1) * 128, :], b[:])
        self.pop()

    def phase_ada(self, c_in, ada_w, ada_b, mod_d, layers=(0, 1)):
        P = self.P
        self.push()
        ct = self.sb("ct", [128, 16], F32)
        condT = self.sb("condT", [128, 16], BF16)
        brow = Rot([self.sb(f"brow{i}", [1, 512], F32) for i in range(2)])
        orow = Rot([self.sb(f"orow{i}", [1, 512], F32) for i in range(2)])
        pss = Rot([self.ps(f"adaps{i}", [128, 512]) for i in range(2)])
        P.dma("sp", ct[:], c_in.rearrange("(k p) -> p k", p=128), allow_slow_non_contiguous=True)
        P.op("act", lambda e: e.activation(condT[:], ct[:], AF.Silu), reads=[ct], writes=[condT])
        for l in layers:
            for cb in range(24):
                wt = self.wt.next()
                wv = wt[:].rearrange("p (k n) -> p k n", k=16)
                P.dma("pool", wv, ada_w[l, :, cb * 512:(cb + 1) * 512].rearrange("(k p) n -> p k n", p=128), writes=[wt])
                bt = brow.next()
                P.dma("sp", bt[:], ada_b[l:l + 1, cb * 512:(cb + 1) * 512])
                ps = pss.next()
                for k in range(16):
                    P.op("pe", lambda e, k=k, ps=ps, wv=wv: e.matmul(ps[0:1, :], condT[:, k:k + 1], wv[:, k, :], start=(k == 0), stop=(k == 15)),
                         reads=[condT, wt], writes=[ps] if k == 0 else [], wparts=[ps] if k else [])
                ot = orow.next()
                P.op("dve", lambda e, ps=ps, bt=bt, ot=ot: e.tensor_tensor(ot[:], ps[0:1, :], bt[:], ALU.add), reads=[ps, bt], writes=[ot])
                P.dma("sp", mod_d[l:l + 1, cb * 512:(cb + 1) * 512], ot[:], wparts=[mod_d])
        self.pop()

    def bload(self, name, src_row):
        t = self.sb(name, [128, D], F32)
        self.P.dma("sp", t[:], src_row.partition_broadcast(128))
        return t

    def load_norm_consts(self, l, nw, si):
        P = self.P
        mod_d = self.mod_d
        sh = self.bload("shbc", mod_d[l, si * D:(si + 1) * D])
        tmp = self.bload("sctmp", mod_d[l, (si + 1) * D:(si + 2) * D])
        gsc = self.bload("gscbc", nw[l, :])
        P.op("dve", lambda e: e.scalar_tensor_tensor(gsc[:], tmp[:], 1.0, gsc[:], ALU.add, ALU.mult), reads=[tmp, gsc], writes=[gsc])
        return gsc, sh

    def norm_to_hT(self, rows_ap, ntiles, gsc_bc, sh_bc, hT, bufs):
        P = self.P
        xt_r, t1_r, xn_r, st_r, pT_r, eps_t = bufs
        ident_b = self.ident_b
        for t in range(ntiles):
            xt = xt_r.next()
            P.dma("sp", xt[:], rows_ap[t * 128:(t + 1) * 128, :])
            st = st_r.next()
            t1 = t1_r.next()
            P.op("act", lambda e, xt=xt, t1=t1, st=st: e.activation(t1[:], xt[:], AF.Square, accum_out=st[:, 0:1]), reads=[xt], writes=[t1, st])
            P.op("act", lambda e, st=st: e.activation(st[:, 1:2], st[:, 0:1], AF.Sqrt, bias=eps_t[:, 0:1], scale=1.0 / D), reads=[st, eps_t], wparts=[st])
            P.op("dve", lambda e, st=st: e.reciprocal(st[:, 2:3], st[:, 1:2]), reads=[st], wparts=[st])
            P.op("dve", lambda e, xt=xt, t1=t1, st=st: e.scalar_tensor_tensor(t1[:], xt[:], st[:, 2:3], gsc_bc[:], ALU.mult, ALU.mult), reads=[xt, st, gsc_bc], writes=[t1])
            xn = xn_r.next()
            P.op("pool", lambda e, t1=t1, xn=xn: e.tensor_tensor(xn[:], t1[:], sh_bc[:], ALU.add), reads=[t1, sh_bc], writes=[xn])
            for half in range(2):
                pT = pT_r.next()
                for kk in range(8):
                    k = half * 8 + kk
                    P.op("pe", lambda e, k=k, kk=kk, pT=pT, xn=xn: e.transpose(pT[:, kk * 128:(kk + 1) * 128], xn[:, k * 128:(k + 1) * 128], self.ident_b[:]),
                         reads=[xn, self.ident_b], writes=[pT] if kk == 0 else [], wparts=[pT] if kk else [])
                eng = "act" if half == 0 else "dve"
                dst = hT[:, half * 8:(half + 1) * 8, t * 128:(t + 1) * 128]
                src = pT[:].rearrange("p (k n) -> p k n", k=8)
                if eng == "act":
                    P.op("act", lambda e, dst=dst, src=src: e.activation(dst, src, AF.Copy), reads=[pT], wparts=[hT])
                else:
                    P.op("dve", lambda e, dst=dst, src=src: e.tensor_copy(dst, src), reads=[pT], wparts=[hT])

    def norm_bufs(self):
        xt_r = Rot([self.sb(f"nxt{i}", [128, D], F32) for i in range(2)])
        t1_r = Rot([self.sb(f"nt1{i}", [128, D], F32) for i in range(1)])
        xn_r = Rot([self.sb(f"nxn{i}", [128, D], BF16) for i in range(2)])
        st_r = Rot([self.sb(f"nst{i}", [128, 4], F32) for i in range(2)])
        pT_r = Rot([self.ps(f"npT{i}", [128, 1024], BF16) for i in range(2)])
        eps_t = self.sb("epst", [128, 1], F32)
        self.P.op("pool", lambda e: e.memset(eps_t[:], EPS), writes=[eps_t])
        return (xt_r, t1_r, xn_r, st_r, pT_r, eps_t)

    def gemm_tok(self, hT, kc, tiles, w_ap, cols, evac, pss, ncols=512):
        P = self.P
        for ci, c0 in enumerate(cols):
            wt = self.wt.next()
            wv = wt[:, 0:kc * ncols].rearrange("p (k n) -> p k n", k=kc)
            P.dma("pool", wv, w_ap[:, c0:c0 + ncols].rearrange("(k p) n -> p k n", p=128), writes=[wt])
            for t in tiles:
                ps = pss.next()
                for k in range(kc):
                    P.op("pe", lambda e, k=k, t=t, ps=ps, wv=wv: e.matmul(ps[:, 0:ncols], hT[:, k, t * 128:(t + 1) * 128], wv[:, k, :], start=(k == 0), stop=(k == kc - 1)),
                         reads=[hT, wt], writes=[ps] if k == 0 else [], wparts=[ps] if k else [])
                evac(ci, c0, t, ps)

    def gemm_feat(self, hT, kc, ntok, w_ap, c_lo, c_hi, evac, pss, tok0=0, ksplit=1, extra=None):
        P = self.P
        kct = kc // ksplit
        wcols = min(8192 // kct, 512)
        tbs = list(range(0, ntok, 512))
        for w0 in range(c_lo, c_hi, wcols):
            wc = min(wcols, c_hi - w0)
            nch = wc // 128
            banks = {}
            for kh in range(ksplit):
                wt = self.wt.next()
                wv = wt[:, 0:kct * wc].rearrange("p (k n) -> p k n", k=kct)
                P.dma("pool", wv, w_ap[kh * kct * 128:(kh + 1) * kct * 128, w0:w0 + wc].rearrange("(k p) n -> p k n", p=128), writes=[wt])
                for cc in range(nch):
                    if extra is not None:
                        extra((w0 + cc * 128 - c_lo) // 128, wv, cc)
                    for tb in tbs:
                        n = min(512, ntok - tb)
                        if kh == 0:
                            banks[(cc, tb)] = pss.next()
                        ps = banks[(cc, tb)]
                        for k in range(kct):
                            first = (kh == 0 and k == 0)
                            last = (kh == ksplit - 1 and k == kct - 1)
                            P.op("pe", lambda e, k=k, cc=cc, tb=tb, n=n, ps=ps, wv=wv, kh=kh, first=first, last=last: e.matmul(ps[:, 0:n], wv[:, k, cc * 128:(cc + 1) * 128], hT[:, kh * kct + k, tok0 + tb:tok0 + tb + n], start=first, stop=last),
                                 reads=[hT, wt], writes=[ps] if first else [], wparts=[] if first else [ps])
                        if kh == ksplit - 1:
                            evac((w0 + cc * 128 - c_lo) // 128, tb, n, ps)

    def l0_mixer(self, x_ext, w_in, pool_w, pool_scale, w_out, qkv_d, att_d, x1_d, hmask, band, hflag, invc_in):
        P = self.P
        w_in = w_in[0]
        self.push()
        uT = self.sb("uT", [128, 4, 16 + T], F32)
        self.push()
        hT = self.sb("hT", [128, 16, T], BF16)
        nb = self.norm_bufs()
        gsc1_bc, sh1_bc = self.load_norm_consts(0, self.norm_mix, 0)
        pss = Rot([self.ps(f"gps{i}", [128, 512]) for i in range(4)])
        stg = Rot([self.sb(f"stg{i}", [128, 4, 512], BF16) for i in range(2)])
        evc = [0]

        def mk_evac(row0):
            cur = {}

            def evac(ci, c0, t, ps):
                if t % 4 == 0:
                    cur["s"] = stg.next()
                s = cur["s"]
                eng = "act" if evc[0] % 2 == 0 else "dve"
                evc[0] += 1
                if eng == "act":
                    P.op("act", lambda e: e.activation(s[:, t % 4, :], ps[:], AF.Copy), reads=[ps], wparts=[s])
                else:
                    P.op("dve", lambda e: e.tensor_copy(s[:, t % 4, :], ps[:]), reads=[ps], wparts=[s])
                if t % 4 == 3:
                    r0 = row0 + (t - 3) * 128
                    P.dma("sp", qkv_d[r0:r0 + 512, c0:c0 + 512].rearrange("(t p) n -> p t n", p=128), s[:], wparts=[qkv_d])
            return evac

        for pas in range(2):
            self.norm_to_hT(x_ext[pas * 2048:(pas + 1) * 2048, :], 16, gsc1_bc, sh1_bc, hT, nb)
            cols = [c * 512 for c in (range(3, 9) if pas == 0 else range(0, 9))]
            self.gemm_tok(hT, 16, list(range(16)), w_in, cols, mk_evac(pas * 2048), pss)
            if pas == 0:
                def evac_u0(ch, tb, n, ps):
                    P.op("dve", lambda e, ch=ch, ps=ps: e.tensor_scalar(uT[:, ch, 0:16], ps[:, 0:16], hflag[:, 0:1], None, ALU.mult), reads=[ps, hflag], wparts=[uT])
                self.gemm_feat(hT, 16, 16, w_in, QKV_W, IN_W0, evac_u0, pss, tok0=T - 16)
            else:
                def evac_u1(ch, tb, n, ps):
                    P.op("act", lambda e, ch=ch, tb=tb, n=n, ps=ps: e.activation(uT[:, ch, 16 + tb:16 + tb + n], ps[:, 0:n], AF.Copy), reads=[ps], wparts=[uT])
                self.gemm_feat(hT, 16, T, w_in, QKV_W, IN_W0, evac_u1, pss)
        self.pop()
        self.attention(qkv_d, att_d, hmask, band)
        self.push()
        mixT = self.sb("mixT", [128, 8, T], BF16)
        self.pool_phase(uT, mixT, pool_w, pool_scale, invc_in)
        self.merge_phase(att_d, mixT)
        self.P.barrier()
        pss = Rot([self.ps(f"ops{i}", [128, 512]) for i in range(4)])
        xin = Rot([self.sb(f"oxin{i}", [128, 512], F32) for i in range(3)])
        tmpb = Rot([self.sb(f"otmp{i}", [128, 512], F32) for i in range(3)])
        xo = Rot([self.sb(f"oxo{i}", [128, 512], F32) for i in range(3)])
        g1_bc = self.bload("g1bc", self.mod_d[0, 2 * D:3 * D])
        x_ext_ = self.x_ext

        def evac_o(ci, c0, t, ps):
            xi, tm, xx = xin.next(), tmpb.next(), xo.next()
            P.dma("sp", xi[:], x_ext_[HALO + t * 128:HALO + (t + 1) * 128, c0:c0 + 512])
            P.op("dve", lambda e: e.tensor_tensor(tm[:], ps[:], g1_bc[:, c0:c0 + 512], ALU.mult), reads=[ps, g1_bc], writes=[tm])
            P.op("pool", lambda e: e.tensor_tensor(xx[:], tm[:], xi[:], ALU.add), reads=[tm, xi], writes=[xx])
            P.dma("sp", x1_d[t * 128:(t + 1) * 128, c0:c0 + 512], xx[:], wparts=[x1_d])

        self.gemm_tok(mixT, 8, list(range(16)), w_out[0], [0, 512, 1024, 1536], evac_o, pss)
        self.pop()
        self.pop()

    def attention(self, qkv_d, att_d, hmask, band):
        P = self.P
        self.push()
        NB = 3
        qb = Rot([self.sb(f"aq{i}", [128, 512], BF16) for i in range(NB)])
        kb = Rot([self.sb(f"ak{i}", [128, 2, 512], BF16) for i in range(NB)])
        vb = Rot([self.sb(f"av{i}", [128, 2, 512], BF16) for i in range(NB)])
        negm = Rot([self.sb(f"anm{i}", [128, 8], F32) for i in range(NB)])
        den = Rot([self.sb(f"adn{i}", [128, 8], F32) for i in range(NB)])
        ostage = Rot([self.sb(f"aos{i}", [128, 512], F32) for i in range(NB)])
        ofin = Rot([self.sb(f"aof{i}", [128, 520], F32) for i in range(2)])
        small = Rot([self.sb(f"asm{i}", [128, 16], F32) for i in range(2)])
        qkT = Rot([self.sb(f"aqkT{i}", [128, 384], BF16) for i in range(3)])
        sm = Rot([self.sb(f"asmx{i}", [128, 256], F32) for i in range(3)])
        pp = Rot([self.sb(f"app{i}", [128, 256], BF16) for i in range(3)])
        pTs = Rot([self.sb(f"apT{i}", [128, 256], BF16) for i in range(3)])
        pqk = Rot([self.ps(f"apqk{i}", [128, 384], BF16) for i in range(2)])
        psc = Rot([self.ps(f"apsc{i}", [128, 256], F32) for i in range(2)])
        ppT = Rot([self.ps(f"appT{i}", [128, 256], BF16) for i in range(2)])
        po = Rot([self.ps(f"apo{i}", [128, 64], F32) for i in range(2)])
        ident_b = self.ident_b

        blocks = []
        for gi, (win, d) in enumerate(GROUPS):
            nblk = T // (128 * d)
            for r in range(d):
                for n in range(nblk):
                    blocks.append((gi, d, r, n))

        def load(bi):
            gi, d, r, n = blocks[bi]
            q, k, v = qb.next(), kb.next(), vb.next()
            q0 = HALO + n * 128 * d + r
            P.dma("sp", q[:], qkv_d[q0:q0 + 127 * d + 1:d, gi * 512:(gi + 1) * 512])
            k0 = HALO + (n * 128 - 128) * d + r
            ksrc = qkv_d[k0:k0 + 255 * d + 1:d, 1536 + gi * 512:1536 + (gi + 1) * 512].rearrange("(c p) n -> p c n", p=128)
            vsrc = qkv_d[k0:k0 + 255 * d + 1:d, 3072 + gi * 512:3072 + (gi + 1) * 512].rearrange("(c p) n -> p c n", p=128)
            P.dma("sp", k[:], ksrc)
            P.dma("sp", v[:], vsrc)
            return dict(q=q, k=k, v=v, negm=negm.next(), den=den.next(), ost=ostage.next())

        units = []
        for bi in range(len(blocks)):
            for h in range(8):
                units.append((bi, h))
        binfo = {}
        uinfo = {}

        def stage_S(u):
            bi, h = units[u]
            gi, d, r, n = blocks[bi]
            if h == 0:
                if bi == 0:
                    binfo[0] = load(0)
                if bi + 1 < len(blocks):
                    binfo[bi + 1] = load(bi + 1)
            B = binfo[bi]
            if h % 2 == 0:
                hp = h // 2
                pq = pqk.next()
                q, k = B["q"], B["k"]
                P.op("pe", lambda e: e.transpose(pq[:, 0:128], q[:, hp * 128:(hp + 1) * 128], ident_b[:]), reads=[q, ident_b], writes=[pq])
                for c in range(2):
                    P.op("pe", lambda e, c=c: e.transpose(pq[:, 128 + c * 128:256 + c * 128], k[:, c, hp * 128:(hp + 1) * 128], ident_b[:]), reads=[k, ident_b], wparts=[pq])
                qk = qkT.next()
                P.op("dve", lambda e: e.tensor_copy(qk[:], pq[:]), reads=[pq], writes=[qk])
                B["qk"] = qk
            qk = B["qk"]
            bp = 64 * (h % 2)
            sc = psc.next()
            P.op("pe", lambda e: e.matmul(sc[:], qk[bp:bp + 64, 0:128], qk[bp:bp + 64, 128:384], start=True, stop=True), reads=[qk], writes=[sc])
            s = sm.next()
            mask = hmask if n == 0 else band
            P.op("dve", lambda e: e.scalar_tensor_tensor(s[:], sc[:], 0.125, mask[:], ALU.mult, ALU.add), reads=[sc, mask], writes=[s])
            nm, dn = B["negm"], B["den"]
            P.op("dve", lambda e: e.tensor_reduce(nm[:, h:h + 1], s[:], AX.X, ALU.max, negate=True), reads=[s], wparts=[nm])
            p = pp.next()
            P.op("act", lambda e: e.activation(p[:], s[:], AF.Exp, bias=nm[:, h:h + 1], scale=1.0, accum_out=dn[:, h:h + 1]), reads=[s, nm], writes=[p], wparts=[dn])
            uinfo[u] = dict(p=p)

        def stage_P(u):
            bi, h = units[u]
            p = uinfo[u]["p"]
            pt = ppT.next()
            for c in range(2):
                P.op("pe", lambda e, c=c: e.transpose(pt[:, c * 128:(c + 1) * 128], p[:, c * 128:(c + 1) * 128], ident_b[:]), reads=[p, ident_b], writes=[pt] if c == 0 else [], wparts=[pt] if c else [])
            ps_ = pTs.next()
            P.op("act", lambda e: e.activation(ps_[:], pt[:], AF.Copy), reads=[pt], writes=[ps_])
            uinfo[u]["pT"] = ps_

        def stage_V(u):
            bi, h = units[u]
            gi, d, r, n = blocks[bi]
            B = binfo[bi]
            pT_ = uinfo[u]["pT"]
            v = B["v"]
            o = po.next()
            for c in range(2):
                P.op("pe", lambda e, c=c: e.matmul(o[:], pT_[:, c * 128:(c + 1) * 128], v[:, c, h * 64:(h + 1) * 64], start=(c == 0), stop=(c == 1)), reads=[pT_, v], writes=[o] if c == 0 else [], wparts=[o] if c else [])
            ost = B["ost"]
            P.op("dve", lambda e: e.tensor_copy(ost[:, h * 64:(h + 1) * 64], o[:]), reads=[o], wparts=[ost])
            del uinfo[u]
            if h == 7:
                nm, dn = B["negm"], B["den"]
                smt = small.next()
                of = ofin.next()
                P.op("dve", lambda e: e.reciprocal(smt[:, 0:8], dn[:]), reads=[dn], wparts=[smt])
                P.op("act", lambda e: e.activation(smt[:, 8:16], dn[:], AF.Ln), reads=[dn], wparts=[smt])
                P.op("pool", lambda e: e.tensor_tensor(of[:, 0:512].rearrange("p (h e) -> p h e", h=8), ost[:].rearrange("p (h e) -> p h e", h=8), smt[:, 0:8].unsqueeze(2).to_broadcast([128, 8, 64]), ALU.mult), reads=[ost, smt], wparts=[of])
                P.op("dve", lambda e: e.tensor_tensor(of[:, 512:520], smt[:, 8:16], nm[:], ALU.subtract), reads=[smt, nm], wparts=[of])
                t0 = n * 128 * d + r
                P.dma("sp", att_d[gi, t0:t0 + 127 * d + 1:d, :], of[:], wparts=[att_d])
                del binfo[bi]

        N = len(units)
        for i in range(N + 2):
            if i < N:
                stage_S(i)
            if 0 <= i - 1 < N:
                stage_P(i - 1)
            if 0 <= i - 2 < N:
                stage_V(i - 2)
        self.pop()

    def pool_phase(self, uT, mixT, pool_w, pool_scale, invc_in):
        P = self.P
        self.push()
        pw = self.sb("pw", [128, 4, 128], BF16)
        psc_t = self.sb("pscale", [128, 4], F32)
        invc = self.sb("invc", [128, 4, 16], F32)
        sA = self.sb("plA", [128, 16 + T], F32)
        sB = self.sb("plB", [128, 16 + T], F32)
        dT = self.sb("pldT", [128, T], BF16)
        pss = Rot([self.ps(f"plps{i}", [128, 512]) for i in range(2)])
        P.dma("pool", pw[:], pool_w[0].rearrange("g c e -> c g e"))
        P.dma("sp", psc_t[:], pool_scale[0].rearrange("(g p) -> p g", p=128), allow_slow_non_contiguous=True)
        P.dma("sp", invc[:], invc_in[:, :, :])
        L = 16 + T
        for gi, w in enumerate((2, 4, 8, 16)):
            u = uT[:, gi, :]
            src = u
            step = 1
            bufs = [sA, sB]
            bi = 0
            while step < w:
                dst = bufs[bi]
                s_ = src
                P.op("pool", lambda e, dst=dst, s_=s_, step=step: e.tensor_tensor(dst[:, step:L], s_[:, step:L], s_[:, 0:L - step], ALU.add),
                     reads=[uT, sA, sB], writes=[dst])
                src = dst[:]
                bi ^= 1
                step *= 2
            ssum = src
            P.op("dve", lambda e, ssum=ssum, u=u, w=w: e.scalar_tensor_tensor(dT[:, 16:T], ssum[:, 32:L], 1.0 / w, u[:, 32:L], ALU.mult, ALU.subtract),
                 reads=[sA, sB, uT], writes=[dT])
            tmp16 = sA if ssum.name != sA.name else sB
            P.op("dve", lambda e, ssum=ssum, tmp16=tmp16, gi=gi: e.tensor_tensor(tmp16[:, 0:16], ssum[:, 16:32], invc[:, gi, :], ALU.mult), reads=[sA, sB, invc], wparts=[tmp16])
            P.op("dve", lambda e, tmp16=tmp16, u=u: e.tensor_tensor(dT[:, 0:16], tmp16[:, 0:16], u[:, 16:32], ALU.subtract), reads=[sA, sB, uT], wparts=[dT])
            for tb in range(4):
                ps = pss.next()
                P.op("pe", lambda e, gi=gi, tb=tb, ps=ps: e.matmul(ps[:], pw[:, gi, :], dT[:, tb * 512:(tb + 1) * 512], start=True, stop=True), reads=[pw, dT], writes=[ps])
                P.op("act", lambda e, gi=gi, tb=tb, ps=ps: e.activation(mixT[:, 4 + gi, tb * 512:(tb + 1) * 512], ps[:], AF.Copy, scale=psc_t[:, gi:gi + 1]), reads=[ps, psc_t], wparts=[mixT])
        self.pop()

    def merge_phase(self, att_d, mixT):
        P = self.P
        self.push()
        ab = Rot([self.sb(f"mab{i}", [128, 3, 520], F32) for i in range(2)])
        sm = Rot([self.sb(f"msm{i}", [128, 48], F32) for i in range(2)])
        acc = Rot([self.sb(f"macc{i}", [128, 512], F32) for i in range(2)])
        tmp = Rot([self.sb(f"mtmp{i}", [128, 512], F32) for i in range(2)])
        ob = Rot([self.sb(f"mob{i}", [128, 512], BF16) for i in range(2)])
        pT = Rot([self.ps(f"mpT{i}", [128, 512], BF16) for i in range(2)])
        for t in range(T // 128):
            a = ab.next()
            P.dma("sp", a[:], att_d[:, t * 128:(t + 1) * 128, :].rearrange("g p n -> p g n"))
            s = sm.next()
            P.op("dve", lambda e, a=a, s=s: e.tensor_tensor(s[:, 0:8], a[:, 0, 512:520], a[:, 1, 512:520], ALU.max), reads=[a], writes=[s])
            P.op("dve", lambda e, a=a, s=s: e.tensor_tensor(s[:, 0:8], s[:, 0:8], a[:, 2, 512:520], ALU.max), reads=[a, s], wparts=[s])
            for gi in range(3):
                P.op("dve", lambda e, a=a, s=s, gi=gi: e.tensor_tensor(s[:, 8 + gi * 8:16 + gi * 8], a[:, gi, 512:520], s[:, 0:8], ALU.subtract), reads=[a, s], wparts=[s])
            P.op("act", lambda e, s=s: e.activation(s[:, 8:32], s[:, 8:32], AF.Exp), reads=[s], wparts=[s])
            P.op("dve", lambda e, s=s: e.tensor_tensor(s[:, 32:40], s[:, 8:16], s[:, 16:24], ALU.add), reads=[s], wparts=[s])
            P.op("dve", lambda e, s=s: e.tensor_tensor(s[:, 32:40], s[:, 32:40], s[:, 24:32], ALU.add), reads=[s], wparts=[s])
            P.op("dve", lambda e, s=s: e.reciprocal(s[:, 40:48], s[:, 32:40]), reads=[s], wparts=[s])
            for gi in range(3):
                P.op("dve", lambda e, s=s, gi=gi: e.tensor_tensor(s[:, 8 + gi * 8:16 + gi * 8], s[:, 8 + gi * 8:16 + gi * 8], s[:, 40:48], ALU.mult), reads=[s], wparts=[s])
            ac, tm = acc.next(), tmp.next()
            v3 = lambda ap: ap.rearrange("p (h e) -> p h e", h=8)
            wb = lambda s, gi: s[:, 8 + gi * 8:16 + gi * 8].unsqueeze(2).to_broadcast([128, 8, 64])
            P.op("pool", lambda e, a=a, s=s, ac=ac: e.tensor_tensor(v3(ac[:]), v3(a[:, 0, 0:512]), wb(s, 0), ALU.mult), reads=[a, s], writes=[ac])
            P.op("dve", lambda e, a=a, s=s, tm=tm: e.tensor_tensor(v3(tm[:]), v3(a[:, 1, 0:512]), wb(s, 1), ALU.mult), reads=[a, s], writes=[tm])
            P.op("pool", lambda e, ac=ac, tm=tm: e.tensor_tensor(ac[:], ac[:], tm[:], ALU.add), reads=[ac, tm], writes=[ac])
            tm2 = tmp.next()
            P.op("dve", lambda e, a=a, s=s, tm2=tm2: e.tensor_tensor(v3(tm2[:]), v3(a[:, 2, 0:512]), wb(s, 2), ALU.mult), reads=[a, s], writes=[tm2])
            o = ob.next()
            P.op("pool", lambda e, ac=ac, tm2=tm2, o=o: e.tensor_tensor(o[:], ac[:], tm2[:], ALU.add), reads=[ac, tm2], writes=[o])
            pt = pT.next()
            for c in range(4):
                P.op("pe", lambda e, c=c, pt=pt, o=o: e.transpose(pt[:, c * 128:(c + 1) * 128], o[:, c * 128:(c + 1) * 128], self.ident_b[:]), reads=[o, self.ident_b], writes=[pt] if c == 0 else [], wparts=[pt] if c else [])
            P.op("act", lambda e, pt=pt, t=t: e.activation(mixT[:, 0:4, t * 128:(t + 1) * 128], pt[:].rearrange("p (c n) -> p c n", c=4), AF.Copy), reads=[pt], wparts=[mixT])
        self.pop()

    def ffn(self, l, xin_d, xout_d, ffn_w1, ffn_w2, hid_d):
        P = self.P
        self.push()
        hT = self.sb("fhT", [128, 16, T], BF16)
        nb = self.norm_bufs()
        gsc2_bc, sh2_bc = self.load_norm_consts(l, self.norm_ffn, 3)
        self.norm_to_hT(xin_d, 16, gsc2_bc, sh2_bc, hT, nb)
        pss = Rot([self.ps(f"fps{i}", [128, 512]) for i in range(4)])
        rl = Rot([self.sb(f"frl{i}", [128, 512], F32) for i in range(3)])
        hs = Rot([self.sb(f"fhs{i}", [128, 4, 512], BF16) for i in range(2)])
        cur = {}

        def evac1(ch, tb, n, ps):
            r = rl.next()
            if tb == 0:
                cur["h"] = hs.next()
            h_ = cur["h"]
            P.op("act", lambda e: e.activation(r[:], ps[:], AF.Relu), reads=[ps], writes=[r])
            P.op("dve", lambda e: e.tensor_tensor(h_[:, tb // 512, :], r[:], r[:], ALU.mult), reads=[r], wparts=[h_])
            if tb == 1536:
                P.dma("sp", hid_d[ch * 128:(ch + 1) * 128, :], h_[:].rearrange("p a b -> p (a b)"), wparts=[hid_d])

        self.gemm_feat(hT, 16, T, ffn_w1[l], 0, FFN_H, evac1, pss)
        self.pop()
        if self.stage == "hid":
            return
        def loader(th, hidT, TH):
            for jq in range(4):
                P.dma("sp", hidT[:, jq * 16:(jq + 1) * 16, :], hid_d[jq * 2048:(jq + 1) * 2048, th * TH:(th + 1) * TH].rearrange("(j p) t -> p j t", p=128), wparts=[hidT])

        self.gemm2_residual(loader, 64, ffn_w2[l], self.mod_d[l, 5 * D:6 * D], xin_d, xout_d, ksplit=2)

    def gemm2_residual(self, loader, kc, w_ap, g_row, xin_d, xout_d, ksplit=1, npss=6):
        P = self.P
        self.push()
        TH = 1024
        hidT = self.sb("hidT", [128, kc, TH], BF16)
        pss = Rot([self.ps(f"f2ps{i}", [128, 512]) for i in range(npss)])
        ptr = Rot([self.ps(f"f2pt{i}", [128, 512]) for i in range(2)])
        dl = Rot([self.sb(f"f2dl{i}", [128, 512], F32) for i in range(3)])
        xi = Rot([self.sb(f"f2xi{i}", [128, 512], F32) for i in range(3)])
        xo = Rot([self.sb(f"f2xo{i}", [128, 512], F32) for i in range(3)])
        g2T = self.sb("g2T", [128, 16], F32)
        P.dma("sp", g2T[:], g_row.rearrange("(k p) -> p k", p=128), allow_slow_non_contiguous=True)
        ident_f = self.ident_f
        for th in range(2):
            loader(th, hidT, TH)

            def evac2(ch, tb, n, ps, th=th):
                d_ = dl.next()
                P.op("act", lambda e: e.activation(d_[:], ps[:], AF.Copy, scale=g2T[:, ch:ch + 1]), reads=[ps, g2T], writes=[d_])
                pt = ptr.next()
                for c in range(4):
                    P.op("pe", lambda e, c=c: e.transpose(pt[:, c * 128:(c + 1) * 128], d_[:, c * 128:(c + 1) * 128], ident_f[:]), reads=[d_, ident_f], writes=[pt] if c == 0 else [], wparts=[pt] if c else [])
                x_i, x_o = xi.next(), xo.next()
                r0 = th * TH + tb
                P.dma("sp", x_i[:].rearrange("p (c n) -> p c n", c=4), xin_d[r0:r0 + 512, ch * 128:(ch + 1) * 128].rearrange("(c p) n -> p c n", p=128))
                P.op("dve", lambda e: e.tensor_tensor(x_o[:], pt[:], x_i[:], ALU.add), reads=[pt, x_i], writes=[x_o])
                P.dma("sp", xout_d[r0:r0 + 512, ch * 128:(ch + 1) * 128].rearrange("(c p) n -> p c n", p=128), x_o[:].rearrange("p (c n) -> p c n", c=4), wparts=[xout_d])

            self.gemm_feat(hidT, kc, TH, w_ap, 0, D, evac2, pss, ksplit=ksplit)
        self.pop()

    def final_norm(self, xin_d, w_row, out_d):
        P = self.P
        self.push()
        wbc = self.bload("fnw", w_row)
        xt_r = Rot([self.sb(f"fnx{i}", [128, D], F32) for i in range(2)])
        t1_r = Rot([self.sb(f"fnt{i}", [128, D], F32) for i in range(2)])
        st_r = Rot([self.sb(f"fns{i}", [128, 4], F32) for i in range(2)])
        eps_t = self.sb("fneps", [128, 1], F32)
        P.op("pool", lambda e: e.memset(eps_t[:], EPS), writes=[eps_t])
        for t in range(T // 128):
            xt, t1, st = xt_r.next(), t1_r.next(), st_r.next()
            P.dma("sp", xt[:], xin_d[t * 128:(t + 1) * 128, :])
            P.op("act", lambda e, xt=xt, t1=t1, st=st: e.activation(t1[:], xt[:], AF.Square, accum_out=st[:, 0:1]), reads=[xt], writes=[t1, st])
            P.op("act", lambda e, st=st: e.activation(st[:, 1:2], st[:, 0:1], AF.Sqrt, bias=eps_t[:, 0:1], scale=1.0 / D), reads=[st, eps_t], wparts=[st])
            P.op("dve", lambda e, st=st: e.reciprocal(st[:, 2:3], st[:, 1:2]), reads=[st], wparts=[st])
            P.op("dve", lambda e, xt=xt, t1=t1, st=st: e.scalar_tensor_tensor(t1[:], xt[:], st[:, 2:3], wbc[:], ALU.mult, ALU.mult), reads=[xt, st, wbc], writes=[t1])
            P.dma("sp", out_d[t * 128:(t + 1) * 128, :], t1[:], wparts=[out_d])
        self.pop()


    def l1_consts(self, ssm_dt_bias, ssm_a_log, ssm_d):
        P = self.P
        c = {}
        bl = lambda name, row: self._bload64(name, row)
        c["dtb"] = bl("dtb", ssm_dt_bias[0, :])
        alog = bl("alog", ssm_a_log[0, :])
        c["D"] = bl("dsk", ssm_d[0, :])
        A = self.sb("Abc", [128, 64], F32)
        P.op("act", lambda e: e.activation(A[:], alog[:], AF.Exp), reads=[alog], writes=[A])
        P.op("dve", lambda e: e.tensor_scalar(A[:], A[:], -1.0, None, ALU.mult), reads=[A], writes=[A])
        c["A"] = A
        one = self.sb("onet", [128, 1], F32)
        P.op("pool", lambda e: e.memset(one[:], 1.0), writes=[one])
        c["one"] = one
        k = np.arange(128)[:, None]
        l = np.arange(128)[None, :]
        for name, arr in (("LE", (k <= l)), ("GT", (k > l)), ("ONES", np.ones((128, 128)))):
            t = self.sb(name, [128, 128], F32)
            P.dma("sp", t[:], self.const(arr.astype(np.float32), "c" + name)[:, :])
            c[name] = t
            tb = self.sb(name + "b", [128, 128], BF16)
            P.op("dve", lambda e, t=t, tb=tb: e.tensor_copy(tb[:], t[:]), reads=[t], writes=[tb])
            c[name + "b"] = tb
        return c

    def _bload64(self, name, row):
        t = self.sb(name, [128, 64], F32)
        self.P.dma("sp", t[:], row.partition_broadcast(128))
        return t

    def l1_inproj(self, mode, x_own, x_halo, hflag, ssm_w_in, conv_w, conv_b, C, dtall, xs_d, b_d, bT_d, cT_d, z_d):
        P = self.P
        w_in = ssm_w_in[0]
        ident_f, ident_b = self.ident_f, self.ident_b
        self.push()
        hT = self.sb("l1hT", [128, 16, T], BF16)
        hTh = self.sb("l1hTh", [128, 16, 128], BF16)
        self.push()
        nb = self.norm_bufs()
        gsc, sh = self.load_norm_consts(1, self.norm_mix, 0)
        self.norm_to_hT(x_halo, 1, gsc, sh, hTh, nb)
        self.norm_to_hT(x_own, 16, gsc, sh, hT, nb)
        self.pop()
        pss = Rot([self.ps(f"l1ps{i}", [128, 512]) for i in range(4)])
        ptr = Rot([self.ps(f"l1pt{i}", [128, 512], BF16) for i in range(2)])
        convT = self.sb("convT", [128, 5, 48], F32)
        cwa = self.sb("cwa", [96, 128], F32)
        cwb = self.sb("cwb", [96, 128], F32)
        cbb = self.sb("cbb", [48, 128], F32)
        cwv = conv_w[0].rearrange("k (c p) -> (k c) p", p=128)
        P.dma("sp", cwa[:], cwv[0:96, :])
        P.dma("sp", cwb[:], cwv[96:192, :])
        P.dma("sp", cbb[:], conv_b[0].rearrange("(c p) -> c p", p=128))
        for i, (src, rows) in enumerate(((cwa, 96), (cwb, 96), (cbb, 48))):
            ps = pss.next()
            P.op("pe", lambda e, ps=ps, src=src, rows=rows: e.transpose(ps[:, 0:rows], src[:, :], ident_f[0:rows, 0:rows]), reads=[src, ident_f], writes=[ps])
            nk = rows // 48
            P.op("dve", lambda e, ps=ps, i=i, nk=nk, rows=rows: e.tensor_copy(convT[:, 2 * i:2 * i + nk, :], ps[:, 0:rows].rearrange("p (k c) -> p k c", k=nk)), reads=[ps], wparts=[convT])
        if mode == "full":
            stg = Rot([self.sb(f"l1zs{i}", [128, 4, 512], BF16) for i in range(2)])
            cur = {}
            evc = [0]

            def evac_z(ci, c0, t, ps):
                if t % 4 == 0:
                    cur["s"] = stg.next()
                s_ = cur["s"]
                if evc[0] % 2 == 0:
                    P.op("act", lambda e: e.activation(s_[:, t % 4, :], ps[:], AF.Copy), reads=[ps], wparts=[s_])
                else:
                    P.op("dve", lambda e: e.tensor_copy(s_[:, t % 4, :], ps[:]), reads=[ps], wparts=[s_])
                evc[0] += 1
                if t % 4 == 3:
                    r0 = (t - 3) * 128
                    P.dma("sp", z_d[r0:r0 + 512, c0:c0 + 512].rearrange("(t p) n -> p t n", p=128), s_[:], wparts=[z_d])

            self.gemm_tok(hT, 16, list(range(16)), w_in, [c * 512 for c in range(8)], evac_z, pss)
        tmpa = Rot([self.sb(f"l1dta{i}", [128, 64], F32) for i in range(2)])
        tmpb = Rot([self.sb(f"l1dtb{i}", [128, 64], F32) for i in range(2)])

        def evac_dt(ci, c0, t, ps):
            a, b = tmpa.next(), tmpb.next()
            P.op("dve", lambda e: e.tensor_tensor(a[:], ps[:, 0:64], C["dtb"][:], ALU.add), reads=[ps, C["dtb"]], writes=[a])
            P.op("act", lambda e: e.activation(b[:], a[:], AF.Exp), reads=[a], writes=[b])
            P.op("act", lambda e: e.activation(dtall[:, t, :], b[:], AF.Ln, bias=C["one"][:, 0:1], scale=1.0), reads=[b, C["one"]], wparts=[dtall])

        self.gemm_tok(hT, 16, list(range(16)), w_in, [DI + 6144], evac_dt, pss, ncols=64)
        xc_r = Rot([self.sb(f"l1xc{i}", [128, 3 + T], F32) for i in range(2)])
        acc_r = Rot([self.sb(f"l1acc{i}", [128, T], F32) for i in range(1)])
        xo_r = Rot([self.sb(f"l1xo{i}", [128, T], BF16) for i in range(2)])
        tks = Rot([self.sb(f"l1tk{i}", [128, 16, 256], BF16) for i in range(2)])
        stt = {}
        evc2 = [0]

        def extra(ch, wv, cc):
            xc = xc_r.next()
            stt["xc"] = xc
            ps = pss.next()
            for k in range(16):
                P.op("pe", lambda e, k=k: e.matmul(ps[:, 0:3], wv[:, k, cc * 128:(cc + 1) * 128], hTh[:, k, 125:128], start=(k == 0), stop=(k == 15)),
                     reads=[hTh, wv], writes=[ps] if k == 0 else [], wparts=[ps] if k else [])
            P.op("dve", lambda e: e.tensor_scalar(xc[:, 0:3], ps[:, 0:3], hflag[:, 0:1], None, ALU.mult), reads=[ps, hflag], writes=[xc])

        def finish(ch, xc):
            acc = acc_r.next()
            P.op("dve", lambda e: e.tensor_scalar(acc[:], xc[:, 0:T], convT[:, 0, ch:ch + 1], None, ALU.mult), reads=[xc, convT], writes=[acc])
            for k in range(1, 4):
                P.op("dve", lambda e, k=k: e.scalar_tensor_tensor(acc[:], xc[:, k:T + k], convT[:, k, ch:ch + 1], acc[:], ALU.mult, ALU.add), reads=[xc, convT, acc], writes=[acc])
            xo = xo_r.next()
            P.op("act", lambda e: e.activation(xo[:], acc[:], AF.Silu, bias=convT[:, 4, ch:ch + 1], scale=1.0), reads=[acc, convT], writes=[xo])
            if ch < 40:
                if ch % 2 == 0:
                    stt["tk"] = tks.next()
                tk = stt["tk"]
                for tq in range(4):
                    pt = ptr.next()
                    for c4 in range(4):
                        t_ = tq * 4 + c4
                        P.op("pe", lambda e, c4=c4, t_=t_, pt=pt: e.transpose(pt[:, c4 * 128:(c4 + 1) * 128], xo[:, t_ * 128:(t_ + 1) * 128], ident_b[:]), reads=[xo, ident_b], writes=[pt] if c4 == 0 else [], wparts=[pt] if c4 else [])
                    dst = tk[:, tq * 4:(tq + 1) * 4, (ch % 2) * 128:(ch % 2 + 1) * 128]
                    src = pt[:].rearrange("p (c n) -> p c n", c=4)
                    if evc2[0] % 2 == 0:
                        P.op("act", lambda e, dst=dst, src=src: e.activation(dst, src, AF.Copy), reads=[pt], wparts=[tk])
                    else:
                        P.op("dve", lambda e, dst=dst, src=src: e.tensor_copy(dst, src), reads=[pt], wparts=[tk])
                    evc2[0] += 1
                if ch % 2 == 1:
                    if ch < 32:
                        dstd = xs_d[:, (ch - 1) * 128:(ch + 1) * 128]
                        nm = xs_d
                    else:
                        dstd = b_d[:, (ch - 32 - 1) * 128:(ch - 32 + 1) * 128]
                        nm = b_d
                    P.dma("sp", dstd.rearrange("(t p) n -> p t n", p=128), tk[:], wparts=[nm])
            if 32 <= ch < 40:
                P.dma("sp", bT_d[(ch - 32) * 128:(ch - 31) * 128, :], xo[:], wparts=[bT_d])
            if ch >= 40:
                P.dma("sp", cT_d[(ch - 40) * 128:(ch - 39) * 128, :], xo[:], wparts=[cT_d])

        def evac_x(ch, tb, n, ps):
            xc = stt["xc"]
            P.op("act", lambda e: e.activation(xc[:, 3 + tb:3 + tb + n], ps[:, 0:n], AF.Copy), reads=[ps], wparts=[xc])
            if tb + n == T:
                finish(ch, xc)

        c_hi = DI + 6144 if mode == "full" else DI + DI + 1024
        self.gemm_feat(hT, 16, T, w_in, DI, c_hi, evac_x, pss, extra=extra)
        self.pop()

    def l1_scan(self, mode, C, dtall, state, state_bf, Lacc, xs_d, b_d, bT_d, cT_d, z_d, yn_d, gn_bc):
        P = self.P
        full = mode == "full"
        self.push()
        LE, GT, ONES = C["LE"], C["GT"], C["ONES"]
        LEb, GTb, ONESb = C["LEb"], C["GTb"], C["ONESb"]
        spl_r = Rot([self.sb(f"sspl{i}", [128, 3, 64], BF16) for i in range(2)])
        rsd_r = Rot([self.sb(f"srsd{i}", [128, 2, 64], F32) for i in range(2)])
        xcb = Rot([self.sb(f"sx{i}", [128, DI], BF16) for i in range(2)])
        bcb = Rot([self.sb(f"sb{i}", [128, 1024], BF16) for i in range(2)])
        dta = Rot([self.sb(f"sdta{i}", [128, 64], F32) for i in range(2)])
        exb = Rot([self.sb(f"sex{i}", [128, 192], F32) for i in range(2)])
        xdt_r = Rot([self.sb(f"sxdt{i}", [128, DI], BF16) for i in range(2)])
        xdtd_r = Rot([self.sb(f"sxdd{i}", [128, DI], BF16) for i in range(1)])
        stmp = Rot([self.sb(f"sstm{i}", [128, 512], F32) for i in range(2)])
        sm_ps = Rot([self.ps(f"ssm{i}", [128, 192]) for i in range(1)])
        st_ps = Rot([self.ps(f"sst{i}", [128, 512]) for i in range(2 if not full else 1)])
        eps_t = self.sb("seps", [128, 1], F32)
        P.op("pool", lambda e: e.memset(eps_t[:], EPS), writes=[eps_t])
        if full:
            zcb = Rot([self.sb(f"sz{i}", [128, DI], BF16) for i in range(2)])
            btb = Rot([self.sb(f"sbt{i}", [128, 8, 128], BF16) for i in range(2)])
            ctb = Rot([self.sb(f"sct{i}", [128, 8, 128], BF16) for i in range(2)])
            cbm_r = Rot([self.sb(f"scbm{i}", [128, 128], F32) for i in range(2)])
            rhs_r = Rot([self.sb(f"srhs{i}", [128, 2, 8, 128], BF16) for i in range(1)])
            E_r = Rot([self.sb(f"sE{i}", [128, 8, 128], F32) for i in range(1)])
            MT_r = Rot([self.sb(f"sMT{i}", [128, 8, 128], BF16) for i in range(2)])
            f512 = Rot([self.sb(f"sf{i}", [128, 512], F32) for i in range(8)])
            yns = Rot([self.sb(f"syn{i}", [128, DI], BF16) for i in range(1)])
            ssq = Rot([self.sb(f"sss{i}", [128, 4], F32) for i in range(4)])
            cb_ps = Rot([self.ps(f"scb{i}", [128, 128]) for i in range(1)])
            seg_ps = Rot([self.ps(f"ssg{i}", [128, 1024]) for i in range(1)])
            yd_ps = Rot([self.ps(f"syd{i}", [128, 512]) for i in range(1)])
            yo_ps = Rot([self.ps(f"syo{i}", [128, 512]) for i in range(1)])
        v3 = lambda ap, h=8: ap.rearrange("p (h e) -> p h e", h=h)
        bc3 = lambda ap, n=64: ap.unsqueeze(2).to_broadcast([128, ap.shape[1], n])
        for c in range(T // 128):
            rows = slice(c * 128, (c + 1) * 128)
            xc = xcb.next()
            P.dma("sp", xc[:], xs_d[rows, :])
            bc = bcb.next()
            P.dma("sp", bc[:], b_d[rows, :])
            if full:
                zc, bt, ct = zcb.next(), btb.next(), ctb.next()
                P.dma("sp", zc[:], z_d[rows, :])
                P.dma("sp", bt[:], bT_d[:, rows].rearrange("(g n) s -> n g s", n=128))
                P.dma("sp", ct[:], cT_d[:, rows].rearrange("(g n) s -> n g s", n=128))
                yn = yns.next()
            dtc = dtall[:, c, :]
            dA = dta.next()
            P.op("dve", lambda e, dA=dA, dtc=dtc: e.tensor_tensor(dA[:], dtc, C["A"][:], ALU.mult), reads=[dtall, C["A"]], writes=[dA])
            spl, rsd = spl_r.next(), rsd_r.next()
            P.op("act", lambda e, spl=spl, dA=dA: e.activation(spl[:, 0, :], dA[:], AF.Copy), reads=[dA], writes=[spl])
            P.op("dve", lambda e, spl=spl, dA=dA, rsd=rsd: e.tensor_tensor(rsd[:, 0, :], dA[:], spl[:, 0, :], ALU.subtract), reads=[dA, spl], writes=[rsd])
            P.op("act", lambda e, spl=spl, rsd=rsd: e.activation(spl[:, 1, :], rsd[:, 0, :], AF.Copy), reads=[rsd], wparts=[spl])
            P.op("dve", lambda e, spl=spl, rsd=rsd: e.tensor_tensor(rsd[:, 1, :], rsd[:, 0, :], spl[:, 1, :], ALU.subtract), reads=[rsd, spl], wparts=[rsd])
            P.op("act", lambda e, spl=spl, rsd=rsd: e.activation(spl[:, 2, :], rsd[:, 1, :], AF.Copy), reads=[rsd], wparts=[spl])
            sp_ = sm_ps.next()
            for i, M in enumerate((LEb, GTb, ONESb)):
                for j in range(3):
                    P.op("pe", lambda e, i=i, j=j, M=M, sp_=sp_, spl=spl: e.matmul(sp_[:, i * 64:(i + 1) * 64], M[:], spl[:, j, :], start=(j == 0), stop=(j == 2)), reads=[M, spl], writes=[sp_] if (i == 0 and j == 0) else [], wparts=[] if (i == 0 and j == 0) else [sp_])
            ex = exb.next()
            P.op("act", lambda e, ex=ex, sp_=sp_: e.activation(ex[:], sp_[:], AF.Exp), reads=[sp_], writes=[ex])
            if not full:
                P.op("dve", lambda e, sp_=sp_: e.tensor_tensor(Lacc[:], Lacc[:], sp_[:, 128:192], ALU.add), reads=[Lacc, sp_], writes=[Lacc])
            if "s1" in os.environ.get("L1DBG", ""):
                continue
            xdt, xdtd = xdt_r.next(), xdtd_r.next()
            P.op("dve", lambda e, xdt=xdt, xc=xc, dtc=dtc: e.tensor_tensor(v3(xdt[:], 64), v3(xc[:], 64), bc3(dtc), ALU.mult), reads=[xc, dtall], writes=[xdt])
            P.op("pool", lambda e, xdt=xdt, xdtd=xdtd, ex=ex: e.tensor_tensor(v3(xdtd[:], 64), v3(xdt[:], 64), bc3(ex[:, 64:128]), ALU.mult), reads=[xdt, ex], writes=[xdtd])
            if "s2" in os.environ.get("L1DBG", ""):
                continue
            for g in range(8):
                gs = slice(g * 512, (g + 1) * 512)
                hs = slice(g * 8, (g + 1) * 8)
                if full:
                    cbp = cb_ps.next()
                    P.op("pe", lambda e, cbp=cbp, bt=bt, ct=ct, g=g: e.matmul(cbp[:], bt[:, g, :], ct[:, g, :], start=True, stop=True), reads=[bt, ct], writes=[cbp])
                    cbm = cbm_r.next()
                    P.op("dve", lambda e, cbm=cbm, cbp=cbp: e.tensor_tensor(cbm[:], cbp[:], LE[:], ALU.mult), reads=[cbp, LE], writes=[cbm])
                    rh = rhs_r.next()
                    for j in range(2):
                        P.op("pool", lambda e, rh=rh, spl=spl, hs=hs, j=j: e.tensor_tensor(rh[:, j], LEb[:].unsqueeze(1).to_broadcast([128, 8, 128]), bc3(spl[:, j, hs], 128), ALU.mult), reads=[LEb, spl], writes=[rh] if j == 0 else [], wparts=[rh] if j else [])
                    sg = seg_ps.next()
                    for hh in range(2):
                        for j in range(2):
                            P.op("pe", lambda e, sg=sg, rh=rh, hh=hh, j=j: e.matmul(sg[:, hh * 512:(hh + 1) * 512], GTb[:], rh[:, j, hh * 4:(hh + 1) * 4, :].rearrange("p r l -> p (r l)"), start=(j == 0), stop=(j == 1)), reads=[GTb, rh], writes=[sg] if (hh == 0 and j == 0) else [], wparts=[] if (hh == 0 and j == 0) else [sg])
                    E = E_r.next()
                    P.op("act", lambda e, E=E, sg=sg: e.activation(E[:].rearrange("p r l -> p (r l)"), sg[:], AF.Exp), reads=[sg], writes=[E])
                    MT = MT_r.next()
                    P.op("dve", lambda e, MT=MT, E=E, cbm=cbm: e.tensor_tensor(MT[:], E[:], cbm[:].unsqueeze(1).to_broadcast([128, 8, 128]), ALU.mult), reads=[E, cbm], writes=[MT])
                    yd = yd_ps.next()
                    for r in range(8):
                        hcol = slice((g * 8 + r) * 64, (g * 8 + r + 1) * 64)
                        P.op("pe", lambda e, yd=yd, MT=MT, xdt=xdt, r=r, hcol=hcol: e.matmul(yd[:, r * 64:(r + 1) * 64], MT[:, r, :], xdt[:, hcol], start=(r == 0), stop=(r == 7)), reads=[MT, xdt], writes=[yd] if r == 0 else [], wparts=[yd] if r else [])
                    yo = yo_ps.next()
                    P.op("pe", lambda e, yo=yo, ct=ct, g=g, gs=gs: e.matmul(yo[:], ct[:, g, :], state_bf[:, gs], start=True, stop=True), reads=[ct, (state_bf, g)], writes=[yo])
                    t1, t2, t3, y, sz, yg, sq = (f512.next() for _ in range(7))
                    P.op("dve", lambda e, t1=t1, yo=yo, ex=ex, hs=hs: e.tensor_tensor(v3(t1[:]), v3(yo[:]), bc3(ex[:, hs]), ALU.mult), reads=[yo, ex], writes=[t1])
                    P.op("pool", lambda e, t2=t2, xc=xc, gs=gs, hs=hs: e.tensor_tensor(v3(t2[:]), v3(xc[:, gs]), bc3(C["D"][:, hs]), ALU.mult), reads=[xc, C["D"]], writes=[t2])
                    P.op("dve", lambda e, t3=t3, yd=yd, t1=t1: e.tensor_tensor(t3[:], yd[:], t1[:], ALU.add), reads=[yd, t1], writes=[t3])
                    P.op("pool", lambda e, y=y, t3=t3, t2=t2: e.tensor_tensor(y[:], t3[:], t2[:], ALU.add), reads=[t3, t2], writes=[y])
                    P.op("act", lambda e, sz=sz, zc=zc, gs=gs: e.activation(sz[:], zc[:, gs], AF.Silu), reads=[zc], writes=[sz])
                    P.op("pool", lambda e, yg=yg, y=y, sz=sz: e.tensor_tensor(yg[:], y[:], sz[:], ALU.mult), reads=[y, sz], writes=[yg])
                    ss = ssq.next()
                    P.op("act", lambda e, sq=sq, yg=yg, ss=ss: e.activation(sq[:], yg[:], AF.Square, accum_out=ss[:, 0:1]), reads=[yg], writes=[sq, ss])
                    P.op("act", lambda e, ss=ss: e.activation(ss[:, 1:2], ss[:, 0:1], AF.Sqrt, bias=eps_t[:, 0:1], scale=1.0 / 512), reads=[ss, eps_t], wparts=[ss])
                    P.op("dve", lambda e, ss=ss: e.reciprocal(ss[:, 2:3], ss[:, 1:2]), reads=[ss], wparts=[ss])
                    P.op("dve", lambda e, yn=yn, yg=yg, ss=ss, gs=gs: e.scalar_tensor_tensor(yn[:, gs], yg[:], ss[:, 2:3], gn_bc[:, gs], ALU.mult, ALU.mult), reads=[yg, ss, gn_bc], wparts=[yn])
                sp2 = st_ps.next()
                P.op("pe", lambda e, sp2=sp2, bc=bc, xdtd=xdtd, g=g, gs=gs: e.matmul(sp2[:], bc[:, g * 128:(g + 1) * 128], xdtd[:, gs], start=True, stop=True), reads=[bc, xdtd], writes=[sp2])
                tm = stmp.next()
                P.op("pool", lambda e, tm=tm, ex=ex, gs=gs, hs=hs: e.tensor_tensor(v3(tm[:]), v3(state[:, gs]), bc3(ex[:, 128 + hs.start:128 + hs.stop]), ALU.mult), reads=[(state, g), ex], writes=[tm])
                P.op("dve", lambda e, tm=tm, sp2=sp2, gs=gs: e.tensor_tensor(state[:, gs], tm[:], sp2[:], ALU.add), reads=[tm, sp2], writes=[(state, g)])
                if full:
                    P.op("act", lambda e, gs=gs: e.activation(state_bf[:, gs], state[:, gs], AF.Copy), reads=[(state, g)], writes=[(state_bf, g)])
            if full:
                P.dma("sp", yn_d[rows, :], yn[:], wparts=[yn_d])
        self.pop()

    def l1_combine(self, Spred, Lpred, state, state_bf):
        P = self.P
        self.push()
        l1 = self._bload64("cl1", Lpred[0, :])
        l2 = self._bload64("cl2", Lpred[1, :])
        c2 = self.sb("cc2", [128, 64], F32)
        c3 = self.sb("cc3", [128, 64], F32)
        P.op("act", lambda e: e.activation(c2[:], l1[:], AF.Exp), reads=[l1], writes=[c2])
        P.op("dve", lambda e: e.tensor_tensor(l2[:], l2[:], l1[:], ALU.add), reads=[l1, l2], writes=[l2])
        P.op("act", lambda e: e.activation(c3[:], l2[:], AF.Exp), reads=[l2], writes=[c3])
        tmp = self.sb("ctmp", [128, DI], F32)
        v3 = lambda ap: ap.rearrange("p (h e) -> p h e", h=64)
        bc3 = lambda ap: ap.unsqueeze(2).to_broadcast([128, 64, 64])
        P.dma("sp", state[:], Spred[0, :, :])
        for j, cj in ((1, c2), (2, c3)):
            P.dma("sp", tmp[:], Spred[j, :, :])
            P.op("pool", lambda e, cj=cj: e.tensor_tensor(v3(tmp[:]), v3(tmp[:]), bc3(cj[:]), ALU.mult), reads=[tmp, cj], writes=[tmp])
            P.op("dve", lambda e: e.tensor_tensor(state[:], state[:], tmp[:], ALU.add), reads=[state, tmp], writes=[state])
        P.op("act", lambda e: e.activation(state_bf[:], state[:], AF.Copy), reads=[state], writes=[state_bf])
        self.pop()

    def l1_outproj(self, yn_d, w_out, xin_d, xout_d):
        P = self.P
        ident_b = self.ident_b
        holder = {}

        def loader(th, hidT, TH):
            if "bufs" not in holder:
                holder["bufs"] = (Rot([self.sb(f"opy{i}", [128, DI], BF16) for i in range(2)]),
                                  Rot([self.ps(f"oppt{i}", [128, 1024], BF16) for i in range(2)]))
            ybuf, ptr = holder["bufs"]
            for t in range(TH // 128):
                yb = ybuf.next()
                r0 = th * TH + t * 128
                P.dma("sp", yb[:], yn_d[r0:r0 + 128, :])
                for q4 in range(4):
                    pt = ptr.next()
                    for kk in range(8):
                        k = q4 * 8 + kk
                        P.op("pe", lambda e, pt=pt, yb=yb, k=k, kk=kk: e.transpose(pt[:, kk * 128:(kk + 1) * 128], yb[:, k * 128:(k + 1) * 128], ident_b[:]), reads=[yb, ident_b], writes=[pt] if kk == 0 else [], wparts=[pt] if kk else [])
                    dst = hidT[:, q4 * 8:(q4 + 1) * 8, t * 128:(t + 1) * 128]
                    src = pt[:].rearrange("p (k n) -> p k n", k=8)
                    if q4 % 2 == 0:
                        P.op("act", lambda e, dst=dst, src=src: e.activation(dst, src, AF.Copy), reads=[pt], wparts=[hidT])
                    else:
                        P.op("dve", lambda e, dst=dst, src=src: e.tensor_copy(dst, src), reads=[pt], wparts=[hidT])

        self.gemm2_residual(loader, 32, w_out[0], self.mod_d[1, 2 * D:3 * D], xin_d, xout_d, ksplit=1, npss=4)

_CACHE = {}
_WKEYS = ("ada_w", "ada_b", "norm_mix", "norm_ffn", "ffn_w1", "ffn_w2")
_L0KEYS = ("even_w_in", "pool_w", "pool_scale", "even_w_out")
_L1KEYS = ("ssm_w_in", "ssm_conv_w", "ssm_conv_b", "ssm_dt_bias", "ssm_a_log", "ssm_d")
_L1FKEYS = ("ssm_norm", "ssm_w_out", "final_norm")


def _prog(stage):
    if stage not in _CACHE:
        _CACHE[stage] = Builder(stage).build()
    return _CACHE[stage]


def l0_inputs(core, inputs):
    b, q = core // 4, core % 4
    x = inputs["x"]
    s0 = q * T
    x_ext = np.zeros((HALO + T, D), np.float32)
    x_ext[HALO:] = x[b, s0:s0 + T]
    if q > 0:
        x_ext[:HALO] = x[b, s0 - HALO:s0]
    qi = np.arange(128)[:, None]
    kj = np.arange(256)[None, :]
    dist = qi + 128 - kj
    ok = (dist >= 0) & (dist <= 128)
    if q == 0:
        ok = ok & (kj >= 128)
    hmask = np.where(ok, 0.0, NEG).astype(np.float32)
    hflag = np.full((128, 1), 0.0 if q == 0 else 1.0, np.float32)
    invc = np.zeros((128, 4, 16), np.float32)
    for gi, w in enumerate((2, 4, 8, 16)):
        for t in range(16):
            invc[:, gi, t] = 1.0 / (min(t + 1, w) if q == 0 else w)
    m = dict(x_ext=x_ext, c_in=np.ascontiguousarray(inputs["c"][b]), hmask=hmask, hflag=hflag, invc=invc)
    for k in _WKEYS + _L0KEYS:
        m[k] = inputs[k]
    return m


def l1_inputs(core, inputs, x2, full, S=None, L=None):
    b, q = core // 4, core % 4
    s0 = q * T
    x_halo = np.zeros((128, D), np.float32)
    if q > 0:
        x_halo[:] = x2[b, s0 - 128:s0]
    hflag = np.full((128, 1), 0.0 if q == 0 else 1.0, np.float32)
    m = dict(x_own=np.ascontiguousarray(x2[b, s0:s0 + T]), x_halo=x_halo, hflag=hflag,
             c_in=np.ascontiguousarray(inputs["c"][b]))
    for k in _WKEYS + _L1KEYS:
        m[k] = inputs[k]
    if full:
        for k in _L1FKEYS:
            m[k] = inputs[k]
        Sp = np.zeros((3, 128, DI), np.float32)
        Lp = np.zeros((3, 64), np.float32)
        for j in range(3):
            if q - 1 - j >= 0:
                src = b * 4 + (q - 1 - j)
                Sp[j] = S[src]
                Lp[j] = L[src]
        m["Spred"] = Sp
        m["Lpred"] = Lp
    return m


def _gather(outs, name="out_x"):
    out = np.zeros((2, SEQ, D), np.float32)
    for c in range(NCORE):
        out[c // 4, (c % 4) * T:(c % 4 + 1) * T] = outs[c][name]
    return out


def _run(stage, in_maps):
    res = run_bass_kernel_spmd(_prog(stage), in_maps, core_ids=list(range(len(in_maps))))
    return res.results


def kernel(**inputs):
    inputs = {k: np.asarray(v) for k, v in inputs.items()}
    cores = list(range(NCORE))
    r0 = _run("L0", [l0_inputs(c, inputs) for c in cores])
    x2 = _gather(r0)
    r1 = _run("L1s", [l1_inputs(c, inputs, x2, False) for c in cores])
    S = [r["S_loc"] for r in r1]
    L = [r["Ltot"][0] for r in r1]
    r2 = _run("L1f", [l1_inputs(c, inputs, x2, True, S, L) for c in cores])
    return _gather(r2)
```
